# Optimizing a Trainium2 kernel written in Bass

```python
import jax, jax.numpy as jnp
from jax import lax
import numpy as np

D_MODEL = 2048
BATCH = 16
SEQ = 2048
DEPTH = 1
DEC_BATCH = 32
DEC_SEQ = 32
PAST_LEN = 2048

CHUNK = 64
N_META = 16
MIX_WIDTH = D_MODEL
POOL_WIDTH = MIX_WIDTH // 2
CONV_WIDTH = MIX_WIDTH - POOL_WIDTH
POOL_WINDOWS = (2, 4, 8, 16)
POOL_GROUPS = len(POOL_WINDOWS)
POOL_GC = POOL_WIDTH // POOL_GROUPS
POOL_HIST = max(POOL_WINDOWS) - 1
CONV_HEADS = 8
CONV_K = 3
CONV_HIST = CONV_K - 1
IN_WIDTH = POOL_WIDTH + 3 * CONV_WIDTH
D_FF = 4 * D_MODEL
EPS = 1e-6

kernel_name = "hymba_pool_shortconv_stream_step"


def _rmsnorm(x, g):
    xf = x.astype(jnp.float32)
    r = lax.rsqrt(jnp.mean(xf * xf, axis=-1, keepdims=True) + EPS)
    return (xf * r * g.astype(jnp.float32)).astype(x.dtype)


def _pool_mixer(u, hist, hist_valid, w_pool, scale):
    B, L, C = u.shape
    ext = jnp.concatenate([hist.astype(u.dtype), u], axis=1)
    extf = ext.astype(jnp.float32)
    cs = jnp.concatenate([jnp.zeros((B, 1, C), jnp.float32), jnp.cumsum(extf, axis=1)], axis=1)
    valid = jnp.concatenate([jnp.full((POOL_HIST,), hist_valid, jnp.float32), jnp.ones((L,), jnp.float32)])
    cv = jnp.concatenate([jnp.zeros((1,), jnp.float32), jnp.cumsum(valid)])
    H1 = POOL_HIST + 1
    parts = []
    for g, w in enumerate(POOL_WINDOWS):
        sl = slice(g * POOL_GC, (g + 1) * POOL_GC)
        s = cs[:, H1:H1 + L, sl] - cs[:, H1 - w:H1 - w + L, sl]
        n = cv[H1:H1 + L] - cv[H1 - w:H1 - w + L]
        parts.append(s / n[None, :, None])
    mean = jnp.concatenate(parts, axis=-1)
    d = (mean - u.astype(jnp.float32)).astype(u.dtype).reshape(B, L, POOL_GROUPS, POOL_GC)
    y = jnp.einsum('blgc,gcd->blgd', d, w_pool).reshape(B, L, C) * scale
    return y, ext[:, -POOL_HIST:]


def _short_conv_mixer(bg, cg, v, hist, conv_w):
    z = cg * v
    L = z.shape[1]
    ext = jnp.concatenate([hist.astype(z.dtype), z], axis=1)
    out = ext[:, 0:L] * conv_w[0]
    for k in range(1, CONV_K):
        out = out + ext[:, k:k + L] * conv_w[k]
    return bg * out, ext[:, -CONV_HIST:]


def _layer(x, pool_hist, conv_hist, hist_valid, norm1_g, w_in, w_pool, pool_scale,
           conv_w, w_out, norm2_g, w_up, w_down):
    hn = _rmsnorm(x, norm1_g)
    proj = jnp.einsum('bld,de->ble', hn, w_in)
    u = proj[..., :POOL_WIDTH]
    bg = proj[..., POOL_WIDTH:POOL_WIDTH + CONV_WIDTH]
    cg = proj[..., POOL_WIDTH + CONV_WIDTH:POOL_WIDTH + 2 * CONV_WIDTH]
    v = proj[..., POOL_WIDTH + 2 * CONV_WIDTH:]
    y_pool, new_pool = _pool_mixer(u, pool_hist, hist_valid, w_pool, pool_scale)
    y_conv, new_conv = _short_conv_mixer(bg, cg, v, conv_hist, conv_w)
    mix = jnp.concatenate([y_pool, y_conv], axis=-1)
    x = x + jnp.einsum('ble,ed->bld', mix, w_out)
    hn2 = _rmsnorm(x, norm2_g)
    a = jax.nn.relu(jnp.einsum('bld,df->blf', hn2, w_up))
    x = x + jnp.einsum('blf,fd->bld', a * a, w_down)
    return x, new_pool, new_conv


def setup_inputs(seed: int = 0) -> dict:
    key = jax.random.key(seed)
    ks = jax.random.split(key, 16)
    f32 = jnp.float32
    x_prompt = jax.random.normal(ks[0], (BATCH, SEQ, D_MODEL), f32)
    x_sample = jax.random.normal(ks[1], (DEC_BATCH, DEC_SEQ, D_MODEL), f32)
    cache_pool = jax.random.normal(ks[2], (DEPTH, DEC_BATCH, POOL_HIST, POOL_WIDTH), f32)
    cache_conv = jax.random.normal(ks[3], (DEPTH, DEC_BATCH, CONV_HIST, CONV_WIDTH), f32)
    meta_tokens = jax.random.normal(ks[4], (N_META, D_MODEL), f32)
    norm1_g = 1.0 + 0.05 * jax.random.normal(ks[5], (DEPTH, D_MODEL), f32)
    w_in = jax.random.normal(ks[6], (DEPTH, D_MODEL, IN_WIDTH), f32) * D_MODEL ** -0.5
    w_pool = jax.random.normal(ks[7], (DEPTH, POOL_GROUPS, POOL_GC, POOL_GC), f32) * POOL_GC ** -0.5
    pool_scale = 1.0 + 0.1 * jax.random.normal(ks[8], (DEPTH, POOL_WIDTH), f32)
    conv_w = jax.random.normal(ks[9], (DEPTH, CONV_K, CONV_WIDTH), f32) * CONV_K ** -0.5
    w_out = jax.random.normal(ks[10], (DEPTH, MIX_WIDTH, D_MODEL), f32) * MIX_WIDTH ** -0.5
    norm2_g = 1.0 + 0.05 * jax.random.normal(ks[11], (DEPTH, D_MODEL), f32)
    w_up = jax.random.normal(ks[12], (DEPTH, D_MODEL, D_FF), f32) * D_MODEL ** -0.5
    w_down = jax.random.normal(ks[13], (DEPTH, D_FF, D_MODEL), f32) * D_FF ** -0.5
    final_g = 1.0 + 0.05 * jax.random.normal(ks[14], (D_MODEL,), f32)
    return {"x_prompt": x_prompt, "x_sample": x_sample, "cache_pool": cache_pool,
            "cache_conv": cache_conv, "meta_tokens": meta_tokens, "norm1_g": norm1_g,
            "w_in": w_in, "w_pool": w_pool, "pool_scale": pool_scale, "conv_w": conv_w,
            "w_out": w_out, "norm2_g": norm2_g, "w_up": w_up, "w_down": w_down,
            "final_g": final_g}


def reference(x_prompt, x_sample, cache_pool, cache_conv, meta_tokens, norm1_g, w_in, w_pool,
              pool_scale, conv_w, w_out, norm2_g, w_up, w_down, final_g):
    bp = x_prompt.shape[0]
    meta = jnp.broadcast_to(meta_tokens.astype(x_prompt.dtype)[None], (bp, N_META, x_prompt.shape[-1]))
    hp = jnp.concatenate([meta, x_prompt], axis=1)
    hs = x_sample
    zp_pool = jnp.zeros((bp, POOL_HIST, POOL_WIDTH), x_prompt.dtype)
    zp_conv = jnp.zeros((bp, CONV_HIST, CONV_WIDTH), x_prompt.dtype)
    sp_pool, sp_conv, ss_pool, ss_conv = [], [], [], []
    for l in range(DEPTH):
        hp, p_pool, p_conv = _layer(hp, zp_pool, zp_conv, 0.0, norm1_g[l], w_in[l], w_pool[l],
                                    pool_scale[l], conv_w[l], w_out[l], norm2_g[l], w_up[l], w_down[l])
        hs, s_pool, s_conv = _layer(hs, cache_pool[l], cache_conv[l], 1.0, norm1_g[l], w_in[l], w_pool[l],
                                    pool_scale[l], conv_w[l], w_out[l], norm2_g[l], w_up[l], w_down[l])
        sp_pool.append(p_pool)
        sp_conv.append(p_conv)
        ss_pool.append(s_pool)
        ss_conv.append(s_conv)
    y_prompt = _rmsnorm(hp, final_g)[:, N_META:]
    y_sample = _rmsnorm(hs, final_g)
    state_pool_prompt = jnp.stack(sp_pool, axis=0)
    state_conv_prompt = jnp.stack(sp_conv, axis=0)
    state_pool_sample = jnp.stack(ss_pool, axis=0)
    state_conv_sample = jnp.stack(ss_conv, axis=0)
    return (y_prompt, y_sample, state_pool_prompt, state_conv_prompt, state_pool_sample, state_conv_sample)
```

```python
import numpy as np
from contextlib import ExitStack
import concourse.bass as bass
import concourse.mybir as mybir
from concourse.bass_utils import run_bass_kernel_spmd

F32 = mybir.dt.float32
BF16 = mybir.dt.bfloat16
AF = mybir.ActivationFunctionType
ALU = mybir.AluOpType

P = 128
D = 2048
KC = D // P
EIN = 4096
DFF = 8192
SEQ = 2048
T = 512
NWB = 3
CH = 16 * 512
NCHUNK = 44
EPS = 1e-6
POOL_W = (2, 4, 8, 16)
N_CORES = 8
DEFER_MOD = (1, 4, 7, 10)


class Res:
    __slots__ = ("name", "last_write", "reads")

    def __init__(self, name):
        self.name = name
        self.last_write = None
        self.reads = []


class Sched:
    ENGS = ("pe", "act", "dve", "pool", "sp")

    def __init__(self, nc):
        self.nc = nc
        self.ops = {e: [] for e in self.ENGS}
        self.prog = {}
        self._ctx = []

    def new_sem(self, name):
        cm = self.nc.semaphore(name)
        s = cm.__enter__()
        self._ctx.append(cm)
        return s

    def dma_sem(self, name):
        return [self.new_sem(name), 0]

    def close(self):
        for cm in reversed(self._ctx):
            cm.__exit__(None, None, None)

    def eng_sem(self, e):
        if e not in self.prog:
            self.prog[e] = [self.new_sem("prog_" + e), 0]
        return self.prog[e]

    def op(self, eng, fn, reads=(), writes=(), dma_sem=None, extra_waits=(), signal=True):
        waits = [w for w in extra_waits if w is not None]
        for r in reads:
            if r.last_write is not None:
                waits.append(r.last_write)
        for w in writes:
            if w.last_write is not None:
                waits.append(w.last_write)
            waits.extend(w.reads)
        if dma_sem is not None:
            dma_sem[1] += 16
            tok = (dma_sem[0], dma_sem[1])
            inc = (dma_sem[0], 16)
        elif signal:
            ps = self.eng_sem(eng)
            ps[1] += 1
            tok = (ps[0], ps[1])
            inc = (ps[0], 1)
        else:
            tok = None
            inc = None
        self.ops[eng].append((fn, waits, inc))
        if tok is not None:
            for r in reads:
                r.reads.append(tok)
            for w in writes:
                w.last_write = tok
                w.reads = []
        return tok

    def replay(self, eng, handle):
        waited = {}
        own = self.prog.get(eng, [None])[0]
        for fn, waits, inc in self.ops[eng]:
            need = {}
            for (s, v) in waits:
                if eng == "pe" and s is own:
                    continue
                k = id(s)
                if waited.get(k, 0) >= v:
                    continue
                if k not in need or need[k][1] < v:
                    need[k] = (s, v)
            for k, (s, v) in need.items():
                handle.wait_ge(s, v)
                waited[k] = v
            ins = fn(handle)
            if inc is not None:
                ins.then_inc(inc[0], inc[1])


def build_program():
    nc = bass.Bass("TRN2", target_bir_lowering=False)

    def din(name, shape):
        return nc.dram_tensor(name, shape, F32, kind="ExternalInput").ap()

    def dout(name, shape):
        return nc.dram_tensor(name, shape, F32, kind="ExternalOutput").ap()

    xp = din("xp", [2, SEQ, D])
    xsm = din("xsm", [128, D])
    cpool = din("cpool", [60, 1024])
    cconv = din("cconv", [8, 1024])
    meta = din("meta", [16, D])
    cvec = din("cvec", [64, 128])
    fgin = din("fg", [1, D])
    w_in = din("w_in", [D, EIN])
    w_pool = din("w_pool", [4, 256, 256])
    w_out = din("w_out", [D, D])
    w_up = din("w_up", [D, DFF])
    w_down = din("w_down", [DFF, D])
    yp = dout("yp", [2, SEQ, D])
    ysm = dout("ysm", [128, D])
    spp = dout("spp", [2, 15, 1024])
    scp = dout("scp", [2, 2, 1024])
    sps = dout("sps", [60, 1024])
    scs = dout("scs", [8, 1024])
    wsc = nc.dram_tensor("wsc", [NCHUNK, P, CH], BF16, kind="Internal").ap()

    es = ExitStack()

    def sb(name, shape, dt):
        return es.enter_context(nc.sbuf_tensor(name, shape, dt))

    x_tm = sb("x_tm", [P, 4, D], F32)
    xn_tm = sb("xn_tm", [P, 4, D], BF16)
    xT = sb("xT", [P, KC, T], BF16)
    hid = sb("hid", [P, 32, T], BF16)
    wbuf = sb("wbuf", [P, NWB, CH], BF16)
    scrA = sb("scrA", [P, 4112], F32)
    uext = sb("uext", [P, 3, 528], F32)
    spp_ = sb("spq", [P, 2, 528], F32)
    dT = sb("dT", [P, 8, T], BF16)
    vb = sb("vb", [P, 2, T], F32)
    junk = sb("junk", [P, D], BF16)
    fgb = sb("fgb", [P, D], F32)
    wpool = sb("wpool", [P, 4, 2, 256], BF16)
    idf = sb("idf", [P, P], F32)
    idb = sb("idb", [P, P], BF16)
    cT = sb("cT", [P, 64], F32)
    cstg = sb("cstg", [64, P], F32)
    uh = sb("uh", [P, 8, 120], F32)
    zh = sb("zh", [P, 8, 16], F32)
    uh_in = sb("uh_in", [P, 8, 60], F32)
    zh_in = sb("zh_in", [P, 8, 8], F32)
    uh_meta = sb("uh_meta", [P, 8, 15], F32)
    zh_meta = sb("zh_meta", [P, 8, 2], F32)
    zmc = sb("zmc", [P, 8, 2], F32)
    xTm = sb("xTm", [P, KC, 16], BF16)
    ss = sb("ss", [P, 16], F32)
    ssp = sb("ssp", [P, 16], F32)
    rstd = sb("rstd", [P, 16], F32)

    banks = [es.enter_context(nc.psum_tensor("bank%d" % i, [P, 512], F32)) for i in range(8)]

    stg2 = junk[:].bitcast(F32)[0:8, :]
    stg = vb[:].rearrange("p a b -> p (a b)")
    xs = [scrA[:, 0:2048], scrA[:, 2064:4112]]

    def czb(jj):
        return scrA[:, jj * 516:(jj + 1) * 516]

    def accb(jj):
        return scrA[:, 2064 + jj * 512:2064 + (jj + 1) * 512]

    S = Sched(nc)
    op = S.op

    Rx_tm = [Res("x_tm%d" % b) for b in range(4)]
    Rxn = [Res("xn%d" % b) for b in range(4)]
    RxT = [Res("xT%d" % c) for c in range(KC)]
    Rhid = [Res("hid%d" % c) for c in range(32)]
    Rwbuf = [Res("wbuf%d" % k) for k in range(NWB)]
    Rwsc = [Res("wsc%d" % i) for i in range(NCHUNK)]
    Rcz = [Res("cz%d" % j) for j in range(4)]
    Racc = [Res("acc%d" % j) for j in range(4)]
    Ruext = [Res("uext%d" % k) for k in range(3)]
    Rspq = [Res("spq%d" % k) for k in range(2)]
    RdT = [Res("dT%d" % k) for k in range(8)]
    Rvb = [Res("vb%d" % k) for k in range(2)]
    Rstg = Rvb
    Rjunk = Res("junk")
    Rfgb = Res("fgb")
    Rwpool_g = [Res("wpool%d" % g) for g in range(4)]
    Ridf = Res("idf")
    Ridb = Res("idb")
    RcT = Res("cT")
    Rcstg = Res("cstg")
    Rstg2 = Rjunk
    Ruh = [Res("uh%d" % c) for c in range(8)]
    Rzh = [Res("zh%d" % c) for c in range(8)]
    Ruh_in = Res("uh_in")
    Rzh_in = Res("zh_in")
    Ruhm = [Res("uhm%d" % c) for c in range(8)]
    Rzhm = [Res("zhm%d" % c) for c in range(8)]
    Rzmc = [Res("zmc%d" % c) for c in range(8)]
    RxTm = Res("xTm")
    Rss = Res("ss")
    Rssp = Res("ssp")
    Rssp_b = [Res("ssp_b%d" % b) for b in range(4)]
    Rss2 = [Res("ss2_%d" % b) for b in range(4)]
    Rrstd2 = [Res("rstd2_%d" % b) for b in range(4)]
    Rrstd = Res("rstd")
    Rbank = [Res("bank%d" % i) for i in range(8)]
    Rxs = [Rcz, Racc]

    dwl = [S.dma_sem("dwl%d" % k) for k in range(NWB)]
    dwl_sw = [S.dma_sem("dwlsw%d" % k) for k in range(NWB)]
    dwb = [S.dma_sem("dwb%d" % k) for k in range(NWB)]
    dxs = [S.dma_sem("dxs%d" % k) for k in range(2)]
    dxt = [S.dma_sem("dxt%d" % b) for b in range(4)]
    dst = [S.dma_sem("dst%d" % b) for b in range(4)]
    dstate = [S.dma_sem("dstate%d" % b) for b in range(2)]
    store_tokens = []

    ring = {"i": 0, "n": 7}

    def next_bank():
        b = ring["i"] % ring["n"]
        ring["i"] += 1
        return b

    def align_ring(m):
        if ring["n"] % m == 0:
            ring["i"] = (ring["i"] + m - 1) // m * m

    IN_COLS = [0, 512, 2048, 3072, 1024, 2560, 3584, 1536]

    def chunk_src(i):
        if i < 8:
            c0 = IN_COLS[i]
            return w_in[:, c0:c0 + 512].rearrange("(kc p) c -> p kc c", p=P)
        if i < 12:
            n = i - 8
            return w_out[:, n * 512:(n + 1) * 512].rearrange("(kc p) c -> p kc c", p=P)
        j = i - 12
        h, r = j // 16, j % 16
        if r < 8:
            c0 = h * 4096 + r * 512
            return w_up[:, c0:c0 + 512].rearrange("(kc p) c -> p kc c", p=P)
        r -= 8
        n, q = r // 2, r % 2
        r0 = (h * 32 + q * 16) * P
        return w_down[r0:r0 + 2048, n * 512:(n + 1) * 512].rearrange("(kc p) c -> p kc c", p=P)

    wstate = {"issued": 0, "used": 0}
    NT_TOTAL = 9
    TOTAL_CHUNKS = NT_TOTAL * NCHUNK

    WBT = {_i: (1 if (_i % 11) in DEFER_MOD else 0) for _i in range(NCHUNK)}

    def issue_load(gi):
        tile_i, i = gi // NCHUNK, gi % NCHUNK
        k = gi % NWB
        if tile_i <= WBT[i]:
            op("pool", lambda e: e.dma_start(out=wbuf[:, k, :].rearrange("p (kc c) -> p kc c", c=512),
                                             in_=chunk_src(i)),
               writes=[Rwbuf[k]], dma_sem=dwl_sw[k])
            if tile_i == WBT[i]:
                op("sp", lambda e: e.dma_start(out=wsc[i], in_=wbuf[:, k, :]),
                   reads=[Rwbuf[k]], writes=[Rwsc[i]], dma_sem=dwb[k])
        else:
            op("sp", lambda e: e.dma_start(out=wbuf[:, k, :], in_=wsc[i]),
               reads=[Rwsc[i]], writes=[Rwbuf[k]], dma_sem=dwl[k])

    def next_chunk():
        gi = wstate["used"]
        while wstate["issued"] < min(gi + NWB, TOTAL_CHUNKS):
            issue_load(wstate["issued"])
            wstate["issued"] += 1
        wstate["used"] += 1
        return gi % NWB

    op("pool", lambda e: e.memset(idf[:], 0.0), writes=[Ridf])
    op("pool", lambda e: e.affine_select(out=idf[:], in_=idf[:], compare_op=ALU.not_equal, fill=1.0, base=0,
                                         pattern=[[-1, P]], channel_multiplier=1), reads=[Ridf], writes=[Ridf])
    op("dve", lambda e: e.tensor_copy(out=idb[:], in_=idf[:]), reads=[Ridf], writes=[Ridb])
    while wstate["issued"] < NWB:
        issue_load(wstate["issued"])
        wstate["issued"] += 1

    op("dve", lambda e: e.memset(scrA[:], 0.0), writes=Rcz + Racc)
    op("dve", lambda e: e.memset(uext[:], 0.0), writes=Ruext)
    op("dve", lambda e: e.memset(spp_[:], 0.0), writes=Rspq)
    op("dve", lambda e: e.memset(ss[:], 1.0), writes=[Rss] + Rss2)
    op("dve", lambda e: e.memset(ssp[:], 1.0), writes=[Rssp] + Rssp_b)
    op("dve", lambda e: e.memset(rstd[:], 1.0), writes=[Rrstd] + Rrstd2)
    for b_ in range(2):
        op("act", lambda e, b_=b_: e.dma_start(out=xs[b_], in_=xp[0, b_ * P:(b_ + 1) * P, :]),
           writes=Rxs[b_], dma_sem=dxs[b_])

    dcs = [S.dma_sem("dc%d" % i) for i in range(8)]
    op("act", lambda e: e.dma_start(out=cstg[:], in_=cvec[:]), writes=[Rcstg], dma_sem=dcs[0])
    op("act", lambda e: e.dma_start(out=fgb[:], in_=fgin.partition_broadcast(P)), writes=[Rfgb], dma_sem=dcs[1])
    for g in range(4):
        op("pool", lambda e, g=g: e.dma_start(out=wpool[:, g, :, :],
                                              in_=w_pool[g].rearrange("(kc p) d -> p kc d", p=P)),
           writes=[Rwpool_g[g]], dma_sem=dcs[2 + g])
    op("pe", lambda e: e.transpose(out=banks[7][:, 0:64], in_=cstg[:], identity=idf[0:64, 0:64]),
       reads=[Rcstg, Ridf], writes=[Rbank[7]])
    op("dve", lambda e: e.tensor_copy(out=cT[:], in_=banks[7][:, 0:64]), reads=[Rbank[7]], writes=[RcT])
    G1, G2, PSC, CW = 0, 16, 32, 40

    def seg3(ap2d, nseg, w, lo, hi):
        return ap2d.rearrange("p (s w) -> p s w", s=nseg)[:, :, lo:hi]

    def rstd_from_ss(c0, n, np_=P, r_ss=None, r_rstd=None):
        r_ss = r_ss or Rss
        r_rstd = r_rstd or Rrstd
        op("dve", lambda e: e.tensor_scalar(out=rstd[0:np_, c0:c0 + n], in0=ss[0:np_, c0:c0 + n], scalar1=1.0 / D,
                                            scalar2=EPS, op0=ALU.mult, op1=ALU.add), reads=[r_ss], writes=[r_rstd])
        op("act", lambda e: e.activation(out=rstd[0:np_, c0:c0 + n], in_=rstd[0:np_, c0:c0 + n], func=AF.Sqrt),
           reads=[r_rstd], writes=[r_rstd])
        op("dve", lambda e: e.reciprocal(out=rstd[0:np_, c0:c0 + n], in_=rstd[0:np_, c0:c0 + n]),
           reads=[r_rstd], writes=[r_rstd])

    dmeta = S.dma_sem("dmeta")
    op("act", lambda e: e.dma_start(out=x_tm[0:16, 0, :], in_=meta[:]), writes=[Rx_tm[0]], dma_sem=dmeta)
    op("act", lambda e: e.activation(out=junk[0:16, :], in_=x_tm[0:16, 0, :], func=AF.Square,
                                     accum_out=ss[0:16, 12:13]),
       reads=[Rx_tm[0]], writes=[Rjunk, Rss])
    rstd_from_ss(12, 1, 16)
    op("dve", lambda e: e.tensor_scalar(out=xn_tm[0:16, 0, :], in0=x_tm[0:16, 0, :], scalar1=rstd[0:16, 12:13],
                                        scalar2=None, op0=ALU.mult), reads=[Rx_tm[0], Rrstd], writes=[Rxn[0]])
    b7b = banks[7].bitcast(BF16)

    def meta_tr(e):
        ins = None
        for c in range(KC):
            ins = e.transpose(out=b7b[:, c * 16:(c + 1) * 16], in_=xn_tm[0:16, 0, c * P:(c + 1) * P],
                              identity=idb[0:16, 0:16])
        return ins
    op("pe", meta_tr, reads=[Rxn[0], Ridb], writes=[Rbank[7]])
    for c in range(KC):
        op("dve", lambda e, c=c: e.tensor_scalar(out=xTm[:, c, :], in0=b7b[:, c * 16:(c + 1) * 16],
                                                 scalar1=cT[:, G1 + c:G1 + c + 1], scalar2=None, op0=ALU.mult),
           reads=[Rbank[7], RcT], writes=[RxTm])

    dhist = [S.dma_sem("dhist%d" % i) for i in range(2)]
    op("act", lambda e: e.dma_start(out=stg[0:60, :], in_=cpool[:]), writes=Rstg, dma_sem=dhist[0])
    op("act", lambda e: e.dma_start(out=stg2[:], in_=cconv[:]), writes=[Rstg2], dma_sem=dhist[1])

    def hist_tr(e):
        ins = None
        for c in range(8):
            ins = e.transpose(out=banks[7][:, c * 60:(c + 1) * 60], in_=stg[0:60, c * P:(c + 1) * P],
                              identity=idf[0:60, 0:60])
        return ins
    op("pe", hist_tr, reads=Rstg + [Ridf], writes=[Rbank[7]])
    op("dve", lambda e: e.tensor_copy(out=uh_in[:].rearrange("p c w -> p (c w)"), in_=banks[7][:, 0:480]),
       reads=[Rbank[7]], writes=[Ruh_in])

    def hist_tr2(e):
        ins = None
        for c in range(8):
            ins = e.transpose(out=banks[7][:, c * 8:(c + 1) * 8], in_=stg2[0:8, c * P:(c + 1) * P],
                              identity=idf[0:8, 0:8])
        return ins
    op("pe", hist_tr2, reads=[Rstg2, Ridf], writes=[Rbank[7]])
    op("dve", lambda e: e.tensor_copy(out=zh_in[:].rearrange("p c w -> p (c w)"), in_=banks[7][:, 0:64]),
       reads=[Rbank[7]], writes=[Rzh_in])

    def mk_tile(kind, s_, r0, nrows, first, states, nseg=1, L=None, extra=None):
        srcs = [xp[s_, r0 + b * P:r0 + (b + 1) * P, :] for b in range(nrows // P)]
        dsts = [yp[s_, r0 + b * P:r0 + (b + 1) * P, :] for b in range(nrows // P)]
        if extra is not None:
            srcs.append(extra[0])
            dsts.append(extra[1])
        nblk_ = len(srcs)
        return dict(kind=kind, T=nblk_ * P, nblk=nblk_, nseg=nseg, L=(L or nblk_ * P), first=first,
                    xblk=srcs, yblk=dsts, states=states)

    tiles = []
    for q in range(4):
        st = [(True, 0, 15, spp[0]), (False, 0, 2, scp[0])] if q == 3 else []
        tiles.append(mk_tile("p", 0, q * T, T, q == 0, st))
    for q in range(3):
        tiles.append(mk_tile("p", 1, q * T, T, q == 0, []))
    tiles.append(mk_tile("p", 1, 3 * T, 384, False, []))
    tiles.append(mk_tile("m", 1, 3 * T + 384, 128, False,
                         [(True, 45, 15, spp[1]), (True, 60, 60, sps), (False, 6, 2, scp[1]), (False, 8, 8, scs)],
                         nseg=8, L=32, extra=(xsm, ysm)))

    def sa_load(tc, b):
        if b >= tc["nblk"]:
            return
        k = b % 2
        op("act", lambda e: e.dma_start(out=xs[k], in_=tc["xblk"][b]),
           writes=Rxs[k], dma_sem=dxs[k])

    def sa_compute(tc, b):
        if b >= tc["nblk"]:
            return
        k = b % 2
        op("act", lambda e: e.activation(out=xn_tm[:, b, :], in_=xs[k], func=AF.Square, accum_out=ss[:, b:b + 1]),
           reads=Rxs[k], writes=[Rxn[b], Rss])
        rstd_from_ss(b, 1)
        op("dve", lambda e: e.tensor_scalar(out=xn_tm[:, b, :], in0=xs[k], scalar1=rstd[:, b:b + 1],
                                            scalar2=None, op0=ALU.mult),
           reads=Rxs[k] + [Rrstd], writes=[Rxn[b]])

    def stage_a_norm(tc, preloaded=0):
        for b in range(preloaded, 2):
            sa_load(tc, b)
        for b in range(tc["nblk"]):
            sa_compute(tc, b)
            sa_load(tc, b + 2)

    SA_SCHED = {0: [("l", 0), ("l", 1)], 8: [("c", 0), ("l", 2)], 14: [("c", 1), ("l", 3)], 20: [("c", 2)],
                26: [("c", 3)]}

    def transposes(tc, gofs):
        nblk, Tt = tc["nblk"], tc["T"]
        for c in range(KC):
            bk = next_bank()
            bb = banks[bk].bitcast(BF16)

            def tr(e, c=c, bb=bb):
                ins = None
                for b in range(nblk):
                    ins = e.transpose(out=bb[:, b * P:(b + 1) * P], in_=xn_tm[:, b, c * P:(c + 1) * P], identity=idb[:])
                return ins
            op("pe", tr, reads=[Rxn[b] for b in range(nblk)] + [Ridb], writes=[Rbank[bk]])
            if c % 2 == 0:
                op("dve", lambda e, c=c, bb=bb: e.tensor_scalar(out=xT[:, c, 0:Tt], in0=bb[:, 0:Tt],
                                                                scalar1=cT[:, gofs + c:gofs + c + 1], scalar2=None,
                                                                op0=ALU.mult),
                   reads=[Rbank[bk], RcT], writes=[RxT[c]])
            else:
                op("act", lambda e, c=c, bb=bb: e.mul(out=xT[:, c, 0:Tt], in_=bb[:, 0:Tt],
                                                      mul=cT[:, gofs + c:gofs + c + 1]),
                   reads=[Rbank[bk], RcT], writes=[RxT[c]])

    def emit_tile(ti, tc, next_tc, prev_tail):
        Tt, nblk, nseg, L = tc["T"], tc["nblk"], tc["nseg"], tc["L"]
        EW = nseg * (15 + L)
        ZW = nseg * (2 + L)
        mixT = hid
        is_t0 = (ti == 0)
        is_m = (tc["kind"] == "m")
        if tc["first"]:
            uh_src, zh_src, Ruh_src, Rzh_src = uh_meta, zh_meta, Ruhm, Rzhm
        else:
            uh_src, zh_src, Ruh_src, Rzh_src = uh, zh, Ruh, Rzh

        if ti == 0:
            transposes(tc, G1)
        if prev_tail is not None:
            prev_tail()
        first_after_T = {"v": True}

        def fm_group(k, e_, bk):
            if first_after_T["v"]:
                first_after_T["v"] = False
                for kc in range(KC):
                    op("pe", lambda e, kc=kc: e.matmul(banks[bk][:, 0:Tt],
                                                       lhsT=wbuf[:, k, kc * 512 + e_ * P:kc * 512 + (e_ + 1) * P],
                                                       rhs=xT[:, kc, 0:Tt], start=(kc == 0), stop=(kc == KC - 1)),
                       reads=[Rwbuf[k], RxT[kc]], writes=[Rbank[bk]], signal=(kc == KC - 1))
                return

            def mm(e):
                ins = None
                for kc in range(KC):
                    ins = e.matmul(banks[bk][:, 0:Tt], lhsT=wbuf[:, k, kc * 512 + e_ * P:kc * 512 + (e_ + 1) * P],
                                   rhs=xT[:, kc, 0:Tt], start=(kc == 0), stop=(kc == KC - 1))
                return ins
            op("pe", mm, reads=[Rwbuf[k]] + RxT, writes=[Rbank[bk]])

        def win_group(k, e_, gi):
            if is_t0 and not (16 <= gi < 20 or gi >= 28):
                def mm_meta(e, k=k, e_=e_, gi=gi):
                    ins = None
                    for kc in range(KC):
                        ins = e.matmul(banks[7][:, gi * 16:(gi + 1) * 16],
                                       lhsT=wbuf[:, k, kc * 512 + e_ * P:kc * 512 + (e_ + 1) * P],
                                       rhs=xTm[:, kc, :], start=(kc == 0), stop=(kc == KC - 1))
                    return ins
                op("pe", mm_meta, reads=[Rwbuf[k], RxTm], writes=[Rbank[7]])
            bk = next_bank()

            fm_group(k, e_, bk)
            return bk

        def pool_chunk(c, bk, gi):
            g = c // 2
            w = POOL_W[g]
            sl = c % 3
            ext = uext[:, sl, 0:EW]
            if is_t0:
                op("act", lambda e: e.activation(out=uh_meta[:, c, :], in_=banks[7][:, gi * 16 + 1:gi * 16 + 16],
                                                 func=AF.Copy), reads=[Rbank[7]], writes=[Ruhm[c]])
            ext3 = ext.rearrange("p (s w) -> p s w", s=nseg)
            if is_m:
                op("dve", lambda e: e.tensor_copy(out=ext3[:, 0:1, 0:15],
                                                  in_=uh[:, c, 0:15].rearrange("p (s w) -> p s w", s=1)),
                   reads=[Ruh[c]], writes=[Ruext[sl]])
                op("dve", lambda e: e.tensor_copy(out=ext3[:, 4:8, 0:15],
                                                  in_=uh_in[:, c, 0:60].rearrange("p (s w) -> p s w", s=4)),
                   reads=[Ruh_in], writes=[Ruext[sl]])
            else:
                op("dve", lambda e: e.tensor_copy(out=seg3(ext, nseg, 15 + L, 0, 15),
                                                  in_=uh_src[:, c, 0:nseg * 15].rearrange("p (s w) -> p s w", s=nseg)),
                   reads=[Ruh_src[c]], writes=[Ruext[sl]])
            op("act", lambda e: e.activation(out=seg3(ext, nseg, 15 + L, 15, 15 + L),
                                             in_=banks[bk][:, 0:Tt].rearrange("p (s l) -> p s l", s=nseg),
                                             func=AF.Copy), reads=[Rbank[bk]], writes=[Ruext[sl]])
            if is_m:
                op("dve", lambda e: e.tensor_copy(out=ext3[:, 1:4, 0:15], in_=ext3[:, 0:3, L:L + 15]),
                   reads=[Ruext[sl]], writes=[Ruext[sl]])
            op("dve", lambda e: e.tensor_copy(out=uh[:, c, 0:nseg * 15].rearrange("p (s w) -> p s w", s=nseg),
                                              in_=seg3(ext, nseg, 15 + L, L, 15 + L)),
               reads=[Ruext[sl]], writes=[Ruh[c]])
            cur, Rcur = ext, Ruext[sl]
            sh = 1
            pp = 0
            while sh < w:
                nxt = spp_[:, pp, 0:EW]
                op("dve", lambda e, cur=cur, nxt=nxt, sh=sh: e.tensor_tensor(out=nxt[:, sh:EW], in0=cur[:, sh:EW],
                                                                             in1=cur[:, 0:EW - sh], op=ALU.add),
                   reads=[Rcur], writes=[Rspq[pp]])
                cur, Rcur = nxt, Rspq[pp]
                pp ^= 1
                sh *= 2
            dsl = c
            op("dve", lambda e, cur=cur: e.scalar_tensor_tensor(
                out=dT[:, dsl, 0:Tt].rearrange("p (s l) -> p s l", s=nseg),
                in0=seg3(cur, nseg, 15 + L, 15, 15 + L), scalar=1.0 / w,
                in1=seg3(ext, nseg, 15 + L, 15, 15 + L), op0=ALU.mult, op1=ALU.subtract),
               reads=[Rcur, Ruext[sl]], writes=[RdT[dsl]])

        def pool_mm(g):
            for o in range(2):
                bk = next_bank()

                def mm(e, o=o, bk=bk):
                    ins = None
                    for kc in range(2):
                        ins = e.matmul(banks[bk][:, 0:Tt], lhsT=wpool[:, g, kc, o * P:(o + 1) * P],
                                       rhs=dT[:, 2 * g + kc, 0:Tt], start=(kc == 0), stop=(kc == 1))
                    return ins
                op("pe", mm, reads=[Rwpool_g[g], RdT[2 * g], RdT[2 * g + 1]], writes=[Rbank[bk]])
                c = 2 * g + o
                op("dve", lambda e, bk=bk, c=c: e.tensor_scalar(out=mixT[:, c, 0:Tt], in0=banks[bk][:, 0:Tt],
                                                                scalar1=cT[:, PSC + c:PSC + c + 1], scalar2=None,
                                                                op0=ALU.mult),
                   reads=[Rbank[bk], RcT], writes=[Rhid[c]])

        gi = 0
        for i in range(2):
            k = next_chunk()
            for e_ in range(4):
                bk = win_group(k, e_, gi)
                pool_chunk(i * 4 + e_, bk, gi)
                gi += 1
        pool_pending = [0, 1, 2, 3]
        for jh in range(2):
            k = next_chunk()
            for e_ in range(4):
                j = jh * 4 + e_
                bk = win_group(k, e_, gi)
                zext = czb(e_)[:, 0:ZW]
                if is_t0:
                    op("act", lambda e, j=j, gi=gi: e.activation(out=zmc[:, j, :],
                                                                 in_=banks[7][:, gi * 16 + 14:gi * 16 + 16],
                                                                 func=AF.Copy), reads=[Rbank[7]], writes=[Rzmc[j]])
                op("act", lambda e, zext=zext, bk=bk: e.activation(
                    out=seg3(zext, nseg, 2 + L, 2, 2 + L),
                    in_=banks[bk][:, 0:Tt].rearrange("p (s l) -> p s l", s=nseg), func=AF.Copy),
                   reads=[Rbank[bk]], writes=[Rcz[e_]])
                gi += 1
            k = next_chunk()
            for e_ in range(4):
                j = jh * 4 + e_
                bk = win_group(k, e_, gi)
                zext = czb(e_)[:, 0:ZW]
                if is_t0:
                    op("dve", lambda e, j=j, gi=gi: e.tensor_tensor(out=zh_meta[:, j, :], in0=zmc[:, j, :],
                                                                    in1=banks[7][:, gi * 16 + 14:gi * 16 + 16],
                                                                    op=ALU.mult),
                       reads=[Rbank[7], Rzmc[j]], writes=[Rzhm[j]])
                op("dve", lambda e, zext=zext, bk=bk: e.tensor_tensor(
                    out=seg3(zext, nseg, 2 + L, 2, 2 + L), in0=seg3(zext, nseg, 2 + L, 2, 2 + L),
                    in1=banks[bk][:, 0:Tt].rearrange("p (s l) -> p s l", s=nseg), op=ALU.mult),
                   reads=[Rbank[bk], Rcz[e_]], writes=[Rcz[e_]])
                if is_m:
                    zext3 = zext.rearrange("p (s w) -> p s w", s=nseg)
                    op("dve", lambda e, zext3=zext3, j=j: e.tensor_copy(
                        out=zext3[:, 0:1, 0:2], in_=zh[:, j, 0:2].rearrange("p (s w) -> p s w", s=1)),
                       reads=[Rzh[j]], writes=[Rcz[e_]])
                    op("dve", lambda e, zext3=zext3, j=j: e.tensor_copy(
                        out=zext3[:, 4:8, 0:2], in_=zh_in[:, j, 0:8].rearrange("p (s w) -> p s w", s=4)),
                       reads=[Rzh_in], writes=[Rcz[e_]])
                    op("dve", lambda e, zext3=zext3: e.tensor_copy(out=zext3[:, 1:4, 0:2],
                                                                   in_=zext3[:, 0:3, L:L + 2]),
                       reads=[Rcz[e_]], writes=[Rcz[e_]])
                else:
                    op("dve", lambda e, zext=zext, j=j: e.tensor_copy(
                        out=seg3(zext, nseg, 2 + L, 0, 2),
                        in_=zh_src[:, j, 0:nseg * 2].rearrange("p (s w) -> p s w", s=nseg)),
                       reads=[Rzh_src[j]], writes=[Rcz[e_]])
                op("dve", lambda e, zext=zext, j=j: e.tensor_copy(
                    out=zh[:, j, 0:nseg * 2].rearrange("p (s w) -> p s w", s=nseg),
                    in_=seg3(zext, nseg, 2 + L, L, 2 + L)),
                   reads=[Rcz[e_]], writes=[Rzh[j]])
                ac = accb(e_)
                op("dve", lambda e, zext=zext, ac=ac, j=j: e.tensor_scalar(
                    out=ac[:, 0:ZW - 2], in0=zext[:, 2:ZW], scalar1=cT[:, CW + 16 + j:CW + 16 + j + 1],
                    scalar2=None, op0=ALU.mult), reads=[Rcz[e_], RcT], writes=[Racc[e_]])
                op("dve", lambda e, zext=zext, ac=ac, j=j: e.scalar_tensor_tensor(
                    out=ac[:, 0:ZW - 2], in0=zext[:, 1:ZW - 1], scalar=cT[:, CW + 8 + j:CW + 8 + j + 1],
                    in1=ac[:, 0:ZW - 2], op0=ALU.mult, op1=ALU.add), reads=[Rcz[e_], RcT, Racc[e_]],
                   writes=[Racc[e_]])
                op("dve", lambda e, zext=zext, ac=ac, j=j: e.scalar_tensor_tensor(
                    out=ac[:, 0:ZW - 2], in0=zext[:, 0:ZW - 2], scalar=cT[:, CW + j:CW + j + 1],
                    in1=ac[:, 0:ZW - 2], op0=ALU.mult, op1=ALU.add), reads=[Rcz[e_], RcT, Racc[e_]],
                   writes=[Racc[e_]])
                gi += 1
                if pool_pending:
                    pool_mm(pool_pending.pop(0))
            k = next_chunk()
            for e_ in range(4):
                j = jh * 4 + e_
                bk = win_group(k, e_, gi)
                ac = accb(e_)
                if nseg == 1:
                    acv = ac[:, 0:Tt].rearrange("p (s l) -> p s l", s=1)
                else:
                    acv = ac[:, 0:nseg * (2 + L)].rearrange("p (s w) -> p s w", s=nseg)[:, :, 0:L]
                op("dve", lambda e, acv=acv, bk=bk, j=j: e.tensor_tensor(
                    out=mixT[:, 8 + j, 0:Tt].rearrange("p (s l) -> p s l", s=nseg), in0=acv,
                    in1=banks[bk][:, 0:Tt].rearrange("p (s l) -> p s l", s=nseg), op=ALU.mult),
                   reads=[Rbank[bk], Racc[e_]], writes=[Rhid[8 + j]])
                gi += 1
            if jh == 0:
                for b in range(nblk):
                    op("act", lambda e, b=b: e.dma_start(out=x_tm[:, b, :], in_=tc["xblk"][b]),
                       writes=[Rx_tm[b]], dma_sem=dxt[b])

        for (is_u, c0, ncol, sdst) in tc["states"]:
            hsrc, Rh = (uh, Ruh) if is_u else (zh, Rzh)
            sbuf_stage, Rst = (stg, Rstg) if is_u else (stg2, [Rstg2])
            for half in range(2):
                bk = next_bank()

                def tr(e, half=half, bk=bk, hsrc=hsrc, c0=c0, ncol=ncol):
                    ins = None
                    for cc in range(4):
                        c = half * 4 + cc
                        ins = e.transpose(out=banks[bk][0:ncol, cc * P:(cc + 1) * P], in_=hsrc[:, c, c0:c0 + ncol],
                                          identity=idf[:])
                    return ins
                op("pe", tr, reads=[Rh[half * 4 + cc] for cc in range(4)] + [Ridf], writes=[Rbank[bk]])
                op("act", lambda e, half=half, bk=bk, sbuf_stage=sbuf_stage, ncol=ncol: e.activation(
                    out=sbuf_stage[0:ncol, half * 512:(half + 1) * 512], in_=banks[bk][0:ncol, :], func=AF.Copy),
                   reads=[Rbank[bk]], writes=Rst)
            store_tokens.append(op("act", lambda e, sbuf_stage=sbuf_stage, ncol=ncol, sdst=sdst: e.dma_start(
                out=sdst, in_=sbuf_stage[0:ncol, :]), reads=Rst, dma_sem=dstate[0 if is_u else 1]))

        if not is_t0:
            ring["n"] = 8
        align_ring(4)
        def evac_wout(b, n, bk):
            op("dve", lambda e: e.tensor_tensor(
                out=x_tm[:, b, n * 512:(n + 1) * 512], in0=x_tm[:, b, n * 512:(n + 1) * 512],
                in1=banks[bk][:, :], op=ALU.add), reads=[Rbank[bk], Rx_tm[b]], writes=[Rx_tm[b]])
            op("act", lambda e: e.activation(out=xn_tm[:, b, n * 512:(n + 1) * 512],
                                             in_=x_tm[:, b, n * 512:(n + 1) * 512], func=AF.Square,
                                             accum_out=ssp[:, b * 4 + n:b * 4 + n + 1]),
               reads=[Rx_tm[b]], writes=[Rxn[b], Rssp_b[b]])

        for n in range(3):
            k = next_chunk()
            bks = [next_bank() for _ in range(nblk)]

            def mm(e, k=k, bks=bks):
                ins = None
                for kc in range(KC):
                    for b in range(nblk):
                        ins = e.matmul(banks[bks[b]][:, :], lhsT=mixT[:, kc, b * P:(b + 1) * P],
                                       rhs=wbuf[:, k, kc * 512:(kc + 1) * 512], start=(kc == 0), stop=(kc == KC - 1))
                return ins
            op("pe", mm, reads=[Rwbuf[k]] + Rhid[0:16], writes=[Rbank[bk] for bk in bks])
            for b in range(nblk):
                evac_wout(b, n, bks[b])
        k = next_chunk()
        for b in range(nblk):
            bk = next_bank()

            def mm(e, k=k, b=b, bk=bk):
                ins = None
                for kc in range(KC):
                    ins = e.matmul(banks[bk][:, :], lhsT=mixT[:, kc, b * P:(b + 1) * P],
                                   rhs=wbuf[:, k, kc * 512:(kc + 1) * 512], start=(kc == 0), stop=(kc == KC - 1))
                return ins
            op("pe", mm, reads=[Rwbuf[k]] + Rhid[0:16], writes=[Rbank[bk]])
            evac_wout(b, 3, bk)
            op("dve", lambda e, b=b: e.tensor_tensor(out=ssp[:, b * 4:b * 4 + 2], in0=ssp[:, b * 4:b * 4 + 2],
                                                     in1=ssp[:, b * 4 + 2:b * 4 + 4], op=ALU.add),
               reads=[Rssp_b[b]], writes=[Rssp_b[b]])
            op("dve", lambda e, b=b: e.tensor_tensor(out=ss[:, 4 + b:5 + b], in0=ssp[:, b * 4:b * 4 + 1],
                                                     in1=ssp[:, b * 4 + 1:b * 4 + 2], op=ALU.add),
               reads=[Rssp_b[b]], writes=[Rss2[b]])
            rstd_from_ss(4 + b, 1, r_ss=Rss2[b], r_rstd=Rrstd2[b])
            if b % 2 == 0:
                op("dve", lambda e, b=b: e.tensor_scalar(out=xn_tm[:, b, :], in0=x_tm[:, b, :],
                                                         scalar1=rstd[:, 4 + b:5 + b], scalar2=None, op0=ALU.mult),
                   reads=[Rx_tm[b], Rrstd2[b]], writes=[Rxn[b]])
            else:
                op("act", lambda e, b=b: e.mul(out=xn_tm[:, b, :], in_=x_tm[:, b, :], mul=rstd[:, 4 + b:5 + b]),
                   reads=[Rx_tm[b], Rrstd2[b]], writes=[Rxn[b]])
        transposes(tc, G2)
        first_after_T["v"] = True

        for h in range(2):
            for m in range(8):
                k = next_chunk()
                for e_ in range(4):
                    hc = m * 4 + e_
                    bk = next_bank()

                    fm_group(k, e_, bk)
                    if h == 0 and next_tc is not None:
                        for kind_, b_ in SA_SCHED.get(hc, []):
                            (sa_load if kind_ == "l" else sa_compute)(next_tc, b_)
                    vs = hc % 2
                    op("act", lambda e, bk=bk, vs=vs: e.activation(out=vb[:, vs, 0:Tt], in_=banks[bk][:, 0:Tt],
                                                                   func=AF.Relu), reads=[Rbank[bk]], writes=[Rvb[vs]])
                    op("dve", lambda e, hc=hc, vs=vs: e.tensor_tensor(out=hid[:, hc, 0:Tt], in0=vb[:, vs, 0:Tt],
                                                                      in1=vb[:, vs, 0:Tt], op=ALU.mult),
                       reads=[Rvb[vs]], writes=[Rhid[hc]])
            if h == 1 and next_tc is not None:
                transposes(next_tc, G1)
            align_ring(4)
            for n in range(4):
                bks = [next_bank() for _ in range(nblk)]
                for q in range(2):
                    k = next_chunk()

                    def mm(e, k=k, q=q, bks=bks):
                        ins = None
                        for kc in range(KC):
                            for b in range(nblk):
                                ins = e.matmul(banks[bks[b]][:, :], lhsT=hid[:, q * 16 + kc, b * P:(b + 1) * P],
                                               rhs=wbuf[:, k, kc * 512:(kc + 1) * 512],
                                               start=(q == 0 and kc == 0), stop=(q == 1 and kc == KC - 1))
                        return ins
                    op("pe", mm, reads=[Rwbuf[k]] + Rhid[q * 16:(q + 1) * 16], writes=[Rbank[bk] for bk in bks])
                for b in range(nblk):
                    op("dve", lambda e, b=b, n=n, bks=bks: e.tensor_tensor(
                        out=x_tm[:, b, n * 512:(n + 1) * 512], in0=x_tm[:, b, n * 512:(n + 1) * 512],
                        in1=banks[bks[b]][:, :], op=ALU.add), reads=[Rbank[bks[b]], Rx_tm[b]], writes=[Rx_tm[b]])

        def tail():
            for b in range(nblk):
                op("act", lambda e, b=b: e.activation(out=junk[:], in_=x_tm[:, b, :], func=AF.Square,
                                                      accum_out=ss[:, 8 + b:9 + b]),
                   reads=[Rx_tm[b]], writes=[Rjunk, Rss])
            rstd_from_ss(8, nblk)
            for b in range(nblk):
                op("dve", lambda e, b=b: e.scalar_tensor_tensor(out=x_tm[:, b, :], in0=x_tm[:, b, :],
                                                                scalar=rstd[:, 8 + b:9 + b], in1=fgb[:],
                                                                op0=ALU.mult, op1=ALU.mult),
                   reads=[Rx_tm[b], Rrstd, Rfgb], writes=[Rx_tm[b]])
                store_tokens.append(op("act", lambda e, b=b: e.dma_start(out=tc["yblk"][b],
                                                                          in_=x_tm[:, b, :]),
                                       reads=[Rx_tm[b]], dma_sem=dst[b]))
        return tail

    stage_a_norm(tiles[0], preloaded=2)
    tail = None
    for ti, tc in enumerate(tiles):
        tail = emit_tile(ti, tc, tiles[ti + 1] if ti + 1 < len(tiles) else None, tail)
    tail()
    assert wstate["used"] == TOTAL_CHUNKS, wstate

    op("sp", lambda e: (e.engine_nop() if hasattr(e, "engine_nop") else e.nop()), extra_waits=store_tokens, signal=False)

    with nc.Block() as block:
        @block.sync
        def _(e):
            S.replay("sp", e)

        @block.gpsimd
        def _(e):
            S.replay("pool", e)

        @block.scalar
        def _(e):
            S.replay("act", e)

        @block.vector
        def _(e):
            S.replay("dve", e)

        @block.tensor
        def _(e):
            S.replay("pe", e)
    S.close()
    es.close()
    return nc


def kernel(x_prompt, x_sample, cache_pool, cache_conv, meta_tokens, norm1_g, w_in, w_pool, pool_scale,
           conv_w, w_out, norm2_g, w_up, w_down, final_g):
    f = lambda a: np.ascontiguousarray(np.asarray(a, dtype=np.float32))
    x_prompt, x_sample, cache_pool, cache_conv = f(x_prompt), f(x_sample), f(cache_pool), f(cache_conv)
    cvec = np.concatenate([f(norm1_g).reshape(16, 128), f(norm2_g).reshape(16, 128),
                           f(pool_scale).reshape(8, 128), f(conv_w).reshape(24, 128)], axis=0)
    shared = {
        "meta": f(meta_tokens), "cvec": np.ascontiguousarray(cvec), "fg": f(final_g).reshape(1, D),
        "w_in": f(w_in)[0], "w_pool": f(w_pool)[0], "w_out": f(w_out)[0], "w_up": f(w_up)[0],
        "w_down": f(w_down)[0],
    }
    in_maps = []
    for c in range(N_CORES):
        m = dict(shared)
        m["xp"] = x_prompt[2 * c:2 * c + 2]
        m["xsm"] = x_sample[4 * c:4 * c + 4].reshape(128, D)
        m["cpool"] = cache_pool[0, 4 * c:4 * c + 4].reshape(60, 1024)
        m["cconv"] = cache_conv[0, 4 * c:4 * c + 4].reshape(8, 1024)
        in_maps.append(m)
    nc = build_program()
    res = run_bass_kernel_spmd(nc, in_maps, core_ids=list(range(N_CORES)))
    R = res.results
    g = lambda name: [np.asarray(R[c][name], dtype=np.float32) for c in range(N_CORES)]
    y_prompt = np.concatenate(g("yp"), axis=0)
    y_sample = np.concatenate([a.reshape(4, 32, D) for a in g("ysm")], axis=0)
    sp_pool = np.concatenate(g("spp"), axis=0)[None]
    sp_conv = np.concatenate(g("scp"), axis=0)[None]
    ss_pool = np.concatenate([a.reshape(4, 15, 1024) for a in g("sps")], axis=0)[None]
    ss_conv = np.concatenate([a.reshape(4, 2, 1024) for a in g("scs")], axis=0)[None]
    return (y_prompt, y_sample, sp_pool, sp_conv, ss_pool, ss_conv)
```

```python
import numpy as np
from contextlib import ExitStack
import concourse.bass as bass
import concourse.mybir as mybir
from concourse.bass_utils import run_bass_kernel_spmd

F32 = mybir.dt.float32
BF16 = mybir.dt.bfloat16
AF = mybir.ActivationFunctionType
ALU = mybir.AluOpType

P = 128
D = 2048
KC = D // P
EIN = 4096
DFF = 8192
SEQ = 2048
T = 512
NWB = 3
CH = 16 * 512
NCHUNK = 44
EPS = 1e-6
POOL_W = (2, 4, 8, 16)
N_CORES = 8
DEFER_MOD = (1, 4, 7, 10)


class Res:
    __slots__ = ("name", "last_write", "reads")

    def __init__(self, name):
        self.name = name
        self.last_write = None
        self.reads = []


class Sched:
    ENGS = ("pe", "act", "dve", "pool", "sp")

    def __init__(self, nc):
        self.nc = nc
        self.ops = {e: [] for e in self.ENGS}
        self.prog = {}
        self._ctx = []

    def new_sem(self, name):
        cm = self.nc.semaphore(name)
        s = cm.__enter__()
        self._ctx.append(cm)
        return s

    def dma_sem(self, name):
        return [self.new_sem(name), 0]

    def close(self):
        for cm in reversed(self._ctx):
            cm.__exit__(None, None, None)

    def eng_sem(self, e):
        if e not in self.prog:
            self.prog[e] = [self.new_sem("prog_" + e), 0]
        return self.prog[e]

    def op(self, eng, fn, reads=(), writes=(), dma_sem=None, extra_waits=(), signal=True):
        waits = [w for w in extra_waits if w is not None]
        for r in reads:
            if r.last_write is not None:
                waits.append(r.last_write)
        for w in writes:
            if w.last_write is not None:
                waits.append(w.last_write)
            waits.extend(w.reads)
        if dma_sem is not None:
            dma_sem[1] += 16
            tok = (dma_sem[0], dma_sem[1])
            inc = (dma_sem[0], 16)
        elif signal:
            ps = self.eng_sem(eng)
            ps[1] += 1
            tok = (ps[0], ps[1])
            inc = (ps[0], 1)
        else:
            tok = None
            inc = None
        self.ops[eng].append((fn, waits, inc))
        if tok is not None:
            for r in reads:
                r.reads.append(tok)
            for w in writes:
                w.last_write = tok
                w.reads = []
        return tok

    def replay(self, eng, handle):
        waited = {}
        own = self.prog.get(eng, [None])[0]
        for fn, waits, inc in self.ops[eng]:
            need = {}
            for (s, v) in waits:
                if eng == "pe" and s is own:
                    continue
                k = id(s)
                if waited.get(k, 0) >= v:
                    continue
                if k not in need or need[k][1] < v:
                    need[k] = (s, v)
            for k, (s, v) in need.items():
                handle.wait_ge(s, v)
                waited[k] = v
            ins = fn(handle)
            if inc is not None:
                ins.then_inc(inc[0], inc[1])


def build_program():
    nc = bass.Bass("TRN2", target_bir_lowering=False)

    def din(name, shape):
        return nc.dram_tensor(name, shape, F32, kind="ExternalInput").ap()

    def dout(name, shape):
        return nc.dram_tensor(name, shape, F32, kind="ExternalOutput").ap()

    xp = din("xp", [2, SEQ, D])
    xsm = din("xsm", [128, D])
    cpool = din("cpool", [60, 1024])
    cconv = din("cconv", [8, 1024])
    meta = din("meta", [16, D])
    cvec = din("cvec", [64, 128])
    fgin = din("fg", [1, D])
    w_in = din("w_in", [D, EIN])
    w_pool = din("w_pool", [4, 256, 256])
    w_out = din("w_out", [D, D])
    w_up = din("w_up", [D, DFF])
    w_down = din("w_down", [DFF, D])
    yp = dout("yp", [2, SEQ, D])
    ysm = dout("ysm", [128, D])
    spp = dout("spp", [2, 15, 1024])
    scp = dout("scp", [2, 2, 1024])
    sps = dout("sps", [60, 1024])
    scs = dout("scs", [8, 1024])
    wsc = nc.dram_tensor("wsc", [NCHUNK, P, CH], BF16, kind="Internal").ap()

    es = ExitStack()

    def sb(name, shape, dt):
        return es.enter_context(nc.sbuf_tensor(name, shape, dt))

    x_tm = sb("x_tm", [P, 4, D], F32)
    xn_tm = sb("xn_tm", [P, 4, D], BF16)
    xT = sb("xT", [P, KC, T], BF16)
    hid = sb("hid", [P, 32, T], BF16)
    wbuf = sb("wbuf", [P, NWB, CH], BF16)
    scrA = sb("scrA", [P, 4112], F32)
    uext = sb("uext", [P, 3, 528], F32)
    spp_ = sb("spq", [P, 2, 528], F32)
    dT = sb("dT", [P, 8, T], BF16)
    vb = sb("vb", [P, 2, T], F32)
    junk = sb("junk", [P, D], BF16)
    fgb = sb("fgb", [P, D], F32)
    wpool = sb("wpool", [P, 4, 2, 256], BF16)
    idf = sb("idf", [P, P], F32)
    idb = sb("idb", [P, P], BF16)
    cT = sb("cT", [P, 64], F32)
    cstg = sb("cstg", [64, P], F32)
    uh = sb("uh", [P, 8, 120], F32)
    zh = sb("zh", [P, 8, 16], F32)
    uh_in = sb("uh_in", [P, 8, 60], F32)
    zh_in = sb("zh_in", [P, 8, 8], F32)
    uh_meta = sb("uh_meta", [P, 8, 15], F32)
    zh_meta = sb("zh_meta", [P, 8, 2], F32)
    zmc = sb("zmc", [P, 8, 2], F32)
    xTm = sb("xTm", [P, KC, 16], BF16)
    ss = sb("ss", [P, 16], F32)
    ssp = sb("ssp", [P, 16], F32)
    rstd = sb("rstd", [P, 16], F32)

    banks = [es.enter_context(nc.psum_tensor("bank%d" % i, [P, 512], F32)) for i in range(8)]

    stg2 = junk[:].bitcast(F32)[0:8, :]
    stg = vb[:].rearrange("p a b -> p (a b)")
    xs = [scrA[:, 0:2048], scrA[:, 2064:4112]]

    def czb(jj):
        return scrA[:, jj * 516:(jj + 1) * 516]

    def accb(jj):
        return scrA[:, 2064 + jj * 512:2064 + (jj + 1) * 512]

    S = Sched(nc)
    op = S.op

    Rx_tm = [Res("x_tm%d" % b) for b in range(4)]
    Rxn = [Res("xn%d" % b) for b in range(4)]
    RxT = [Res("xT%d" % c) for c in range(KC)]
    Rhid = [Res("hid%d" % c) for c in range(32)]
    Rwbuf = [Res("wbuf%d" % k) for k in range(NWB)]
    Rwsc = [Res("wsc%d" % i) for i in range(NCHUNK)]
    Rcz = [Res("cz%d" % j) for j in range(4)]
    Racc = [Res("acc%d" % j) for j in range(4)]
    Ruext = [Res("uext%d" % k) for k in range(3)]
    Rspq = [Res("spq%d" % k) for k in range(2)]
    RdT = [Res("dT%d" % k) for k in range(8)]
    Rvb = [Res("vb%d" % k) for k in range(2)]
    Rstg = Rvb
    Rjunk = Res("junk")
    Rfgb = Res("fgb")
    Rwpool_g = [Res("wpool%d" % g) for g in range(4)]
    Ridf = Res("idf")
    Ridb = Res("idb")
    RcT = Res("cT")
    Rcstg = Res("cstg")
    Rstg2 = Rjunk
    Ruh = [Res("uh%d" % c) for c in range(8)]
    Rzh = [Res("zh%d" % c) for c in range(8)]
    Ruh_in = Res("uh_in")
    Rzh_in = Res("zh_in")
    Ruhm = [Res("uhm%d" % c) for c in range(8)]
    Rzhm = [Res("zhm%d" % c) for c in range(8)]
    Rzmc = [Res("zmc%d" % c) for c in range(8)]
    RxTm = Res("xTm")
    Rss = Res("ss")
    Rssp = Res("ssp")
    Rssp_b = [Res("ssp_b%d" % b) for b in range(4)]
    Rss2 = [Res("ss2_%d" % b) for b in range(4)]
    Rrstd2 = [Res("rstd2_%d" % b) for b in range(4)]
    Rrstd = Res("rstd")
    Rbank = [Res("bank%d" % i) for i in range(8)]
    Rxs = [Rcz, Racc]

    dwl = [S.dma_sem("dwl%d" % k) for k in range(NWB)]
    dwl_sw = [S.dma_sem("dwlsw%d" % k) for k in range(NWB)]
    dwb = [S.dma_sem("dwb%d" % k) for k in range(NWB)]
    dxs = [S.dma_sem("dxs%d" % k) for k in range(2)]
    dxt = [S.dma_sem("dxt%d" % b) for b in range(4)]
    dst = [S.dma_sem("dst%d" % b) for b in range(4)]
    dstate = [S.dma_sem("dstate%d" % b) for b in range(2)]
    store_tokens = []

    ring = {"i": 0, "n": 7}

    def next_bank():
        b = ring["i"] % ring["n"]
        ring["i"] += 1
        return b

    def align_ring(m):
        if ring["n"] % m == 0:
            ring["i"] = (ring["i"] + m - 1) // m * m

    IN_COLS = [0, 512, 2048, 3072, 1024, 2560, 3584, 1536]

    def chunk_src(i):
        if i < 8:
            c0 = IN_COLS[i]
            return w_in[:, c0:c0 + 512].rearrange("(kc p) c -> p kc c", p=P)
        if i < 12:
            n = i - 8
            return w_out[:, n * 512:(n + 1) * 512].rearrange("(kc p) c -> p kc c", p=P)
        j = i - 12
        h, r = j // 16, j % 16
        if r < 8:
            c0 = h * 4096 + r * 512
            return w_up[:, c0:c0 + 512].rearrange("(kc p) c -> p kc c", p=P)
        r -= 8
        n, q = r // 2, r % 2
        r0 = (h * 32 + q * 16) * P
        return w_down[r0:r0 + 2048, n * 512:(n + 1) * 512].rearrange("(kc p) c -> p kc c", p=P)

    wstate = {"issued": 0, "used": 0}
    NT_TOTAL = 9
    TOTAL_CHUNKS = NT_TOTAL * NCHUNK

    WBT = {_i: (1 if (_i % 11) in DEFER_MOD else 0) for _i in range(NCHUNK)}

    def issue_load(gi):
        tile_i, i = gi // NCHUNK, gi % NCHUNK
        k = gi % NWB
        if tile_i <= WBT[i]:
            op("pool", lambda e: e.dma_start(out=wbuf[:, k, :].rearrange("p (kc c) -> p kc c", c=512),
                                             in_=chunk_src(i)),
               writes=[Rwbuf[k]], dma_sem=dwl_sw[k])
            if tile_i == WBT[i]:
                op("sp", lambda e: e.dma_start(out=wsc[i], in_=wbuf[:, k, :]),
                   reads=[Rwbuf[k]], writes=[Rwsc[i]], dma_sem=dwb[k])
        else:
            op("sp", lambda e: e.dma_start(out=wbuf[:, k, :], in_=wsc[i]),
               reads=[Rwsc[i]], writes=[Rwbuf[k]], dma_sem=dwl[k])

    def next_chunk():
        gi = wstate["used"]
        while wstate["issued"] < min(gi + NWB, TOTAL_CHUNKS):
            issue_load(wstate["issued"])
            wstate["issued"] += 1
        wstate["used"] += 1
        return gi % NWB

    op("pool", lambda e: e.memset(idf[:], 0.0), writes=[Ridf])
    op("pool", lambda e: e.affine_select(out=idf[:], in_=idf[:], compare_op=ALU.not_equal, fill=1.0, base=0,
                                         pattern=[[-1, P]], channel_multiplier=1), reads=[Ridf], writes=[Ridf])
    op("dve", lambda e: e.tensor_copy(out=idb[:], in_=idf[:]), reads=[Ridf], writes=[Ridb])
    while wstate["issued"] < NWB:
        issue_load(wstate["issued"])
        wstate["issued"] += 1

    op("dve", lambda e: e.memset(scrA[:], 0.0), writes=Rcz + Racc)
    op("dve", lambda e: e.memset(uext[:], 0.0), writes=Ruext)
    op("dve", lambda e: e.memset(spp_[:], 0.0), writes=Rspq)
    op("dve", lambda e: e.memset(ss[:], 1.0), writes=[Rss] + Rss2)
    op("dve", lambda e: e.memset(ssp[:], 1.0), writes=[Rssp] + Rssp_b)
    op("dve", lambda e: e.memset(rstd[:], 1.0), writes=[Rrstd] + Rrstd2)
    for b_ in range(4):
        op("act", lambda e, b_=b_: e.dma_start(out=x_tm[:, b_, :], in_=xp[0, b_ * P:(b_ + 1) * P, :]),
           writes=[Rx_tm[b_]], dma_sem=dxt[b_])

    dcs = [S.dma_sem("dc%d" % i) for i in range(8)]
    op("act", lambda e: e.dma_start(out=cstg[:], in_=cvec[:]), writes=[Rcstg], dma_sem=dcs[0])
    op("act", lambda e: e.dma_start(out=fgb[:], in_=fgin.partition_broadcast(P)), writes=[Rfgb], dma_sem=dcs[1])
    for g in range(4):
        op("pool", lambda e, g=g: e.dma_start(out=wpool[:, g, :, :],
                                              in_=w_pool[g].rearrange("(kc p) d -> p kc d", p=P)),
           writes=[Rwpool_g[g]], dma_sem=dcs[2 + g])
    op("pe", lambda e: e.transpose(out=banks[7][:, 0:64], in_=cstg[:], identity=idf[0:64, 0:64]),
       reads=[Rcstg, Ridf], writes=[Rbank[7]])
    op("dve", lambda e: e.tensor_copy(out=cT[:], in_=banks[7][:, 0:64]), reads=[Rbank[7]], writes=[RcT])
    G1, G2, PSC, CW = 0, 16, 32, 40

    def seg3(ap2d, nseg, w, lo, hi):
        return ap2d.rearrange("p (s w) -> p s w", s=nseg)[:, :, lo:hi]

    def rstd_from_ss(c0, n, np_=P, r_ss=None, r_rstd=None):
        r_ss = r_ss or Rss
        r_rstd = r_rstd or Rrstd
        op("dve", lambda e: e.tensor_scalar(out=rstd[0:np_, c0:c0 + n], in0=ss[0:np_, c0:c0 + n], scalar1=1.0 / D,
                                            scalar2=EPS, op0=ALU.mult, op1=ALU.add), reads=[r_ss], writes=[r_rstd])
        op("act", lambda e: e.activation(out=rstd[0:np_, c0:c0 + n], in_=rstd[0:np_, c0:c0 + n], func=AF.Sqrt),
           reads=[r_rstd], writes=[r_rstd])
        op("dve", lambda e: e.reciprocal(out=rstd[0:np_, c0:c0 + n], in_=rstd[0:np_, c0:c0 + n]),
           reads=[r_rstd], writes=[r_rstd])

    dmeta = S.dma_sem("dmeta")
    op("act", lambda e: e.dma_start(out=xs[1][0:16, :], in_=meta[:]), writes=Rxs[1], dma_sem=dmeta)
    op("act", lambda e: e.activation(out=junk[0:16, :], in_=xs[1][0:16, :], func=AF.Square,
                                     accum_out=ss[0:16, 12:13]),
       reads=Rxs[1], writes=[Rjunk, Rss])
    rstd_from_ss(12, 1, 16)
    op("dve", lambda e: e.tensor_scalar(out=xn_tm[0:16, 0, :], in0=xs[1][0:16, :], scalar1=rstd[0:16, 12:13],
                                        scalar2=None, op0=ALU.mult), reads=Rxs[1] + [Rrstd], writes=[Rxn[0]])
    b7b = banks[7].bitcast(BF16)

    def meta_tr(e):
        ins = None
        for c in range(KC):
            ins = e.transpose(out=b7b[:, c * 16:(c + 1) * 16], in_=xn_tm[0:16, 0, c * P:(c + 1) * P],
                              identity=idb[0:16, 0:16])
        return ins
    op("pe", meta_tr, reads=[Rxn[0], Ridb], writes=[Rbank[7]])
    for c in range(KC):
        op("dve", lambda e, c=c: e.tensor_scalar(out=xTm[:, c, :], in0=b7b[:, c * 16:(c + 1) * 16],
                                                 scalar1=cT[:, G1 + c:G1 + c + 1], scalar2=None, op0=ALU.mult),
           reads=[Rbank[7], RcT], writes=[RxTm])

    dhist = [S.dma_sem("dhist%d" % i) for i in range(2)]
    op("act", lambda e: e.dma_start(out=stg[0:60, :], in_=cpool[:]), writes=Rstg, dma_sem=dhist[0])
    op("act", lambda e: e.dma_start(out=stg2[:], in_=cconv[:]), writes=[Rstg2], dma_sem=dhist[1])

    def hist_tr(e):
        ins = None
        for c in range(8):
            ins = e.transpose(out=banks[7][:, c * 60:(c + 1) * 60], in_=stg[0:60, c * P:(c + 1) * P],
                              identity=idf[0:60, 0:60])
        return ins
    op("pe", hist_tr, reads=Rstg + [Ridf], writes=[Rbank[7]])
    op("dve", lambda e: e.tensor_copy(out=uh_in[:].rearrange("p c w -> p (c w)"), in_=banks[7][:, 0:480]),
       reads=[Rbank[7]], writes=[Ruh_in])

    def hist_tr2(e):
        ins = None
        for c in range(8):
            ins = e.transpose(out=banks[7][:, c * 8:(c + 1) * 8], in_=stg2[0:8, c * P:(c + 1) * P],
                              identity=idf[0:8, 0:8])
        return ins
    op("pe", hist_tr2, reads=[Rstg2, Ridf], writes=[Rbank[7]])
    op("dve", lambda e: e.tensor_copy(out=zh_in[:].rearrange("p c w -> p (c w)"), in_=banks[7][:, 0:64]),
       reads=[Rbank[7]], writes=[Rzh_in])

    def mk_tile(kind, s_, r0, nrows, first, states, nseg=1, L=None, extra=None):
        srcs = [xp[s_, r0 + b * P:r0 + (b + 1) * P, :] for b in range(nrows // P)]
        dsts = [yp[s_, r0 + b * P:r0 + (b + 1) * P, :] for b in range(nrows // P)]
        if extra is not None:
            srcs.append(extra[0])
            dsts.append(extra[1])
        nblk_ = len(srcs)
        return dict(kind=kind, T=nblk_ * P, nblk=nblk_, nseg=nseg, L=(L or nblk_ * P), first=first,
                    xblk=srcs, yblk=dsts, states=states)

    tiles = []
    for q in range(4):
        st = [(True, 0, 15, spp[0]), (False, 0, 2, scp[0])] if q == 3 else []
        tiles.append(mk_tile("p", 0, q * T, T, q == 0, st))
    for q in range(3):
        tiles.append(mk_tile("p", 1, q * T, T, q == 0, []))
    tiles.append(mk_tile("p", 1, 3 * T, 384, False, []))
    tiles.append(mk_tile("m", 1, 3 * T + 384, 128, False,
                         [(True, 45, 15, spp[1]), (True, 60, 60, sps), (False, 6, 2, scp[1]), (False, 8, 8, scs)],
                         nseg=8, L=32, extra=(xsm, ysm)))

    def sa_load(tc, b):
        if b >= tc["nblk"]:
            return
        k = b % 2
        op("act", lambda e: e.dma_start(out=xs[k], in_=tc["xblk"][b]),
           writes=Rxs[k], dma_sem=dxs[k])

    def sa_compute(tc, b):
        if b >= tc["nblk"]:
            return
        k = b % 2
        op("act", lambda e: e.activation(out=xn_tm[:, b, :], in_=xs[k], func=AF.Square, accum_out=ss[:, b:b + 1]),
           reads=Rxs[k], writes=[Rxn[b], Rss])
        rstd_from_ss(b, 1)
        op("dve", lambda e: e.tensor_scalar(out=xn_tm[:, b, :], in0=xs[k], scalar1=rstd[:, b:b + 1],
                                            scalar2=None, op0=ALU.mult),
           reads=Rxs[k] + [Rrstd], writes=[Rxn[b]])

    def stage_a_norm(tc, preloaded=0):
        for b in range(preloaded, 2):
            sa_load(tc, b)
        for b in range(tc["nblk"]):
            sa_compute(tc, b)
            sa_load(tc, b + 2)

    SA_SCHED = {0: [("l", 0), ("l", 1)], 8: [("c", 0), ("l", 2)], 14: [("c", 1), ("l", 3)], 20: [("c", 2)],
                26: [("c", 3)]}

    def transposes(tc, gofs):
        nblk, Tt = tc["nblk"], tc["T"]
        for c in range(KC):
            bk = next_bank()
            bb = banks[bk].bitcast(BF16)

            def tr(e, c=c, bb=bb):
                ins = None
                for b in range(nblk):
                    ins = e.transpose(out=bb[:, b * P:(b + 1) * P], in_=xn_tm[:, b, c * P:(c + 1) * P], identity=idb[:])
                return ins
            op("pe", tr, reads=[Rxn[b] for b in range(nblk)] + [Ridb], writes=[Rbank[bk]])
            if c % 2 == 0:
                op("dve", lambda e, c=c, bb=bb: e.tensor_scalar(out=xT[:, c, 0:Tt], in0=bb[:, 0:Tt],
                                                                scalar1=cT[:, gofs + c:gofs + c + 1], scalar2=None,
                                                                op0=ALU.mult),
                   reads=[Rbank[bk], RcT], writes=[RxT[c]])
            else:
                op("act", lambda e, c=c, bb=bb: e.mul(out=xT[:, c, 0:Tt], in_=bb[:, 0:Tt],
                                                      mul=cT[:, gofs + c:gofs + c + 1]),
                   reads=[Rbank[bk], RcT], writes=[RxT[c]])

    def emit_tile(ti, tc, next_tc, prev_tail):
        Tt, nblk, nseg, L = tc["T"], tc["nblk"], tc["nseg"], tc["L"]
        EW = nseg * (15 + L)
        ZW = nseg * (2 + L)
        mixT = hid
        is_t0 = (ti == 0)
        is_m = (tc["kind"] == "m")
        if tc["first"]:
            uh_src, zh_src, Ruh_src, Rzh_src = uh_meta, zh_meta, Ruhm, Rzhm
        else:
            uh_src, zh_src, Ruh_src, Rzh_src = uh, zh, Ruh, Rzh

        if ti == 0:
            transposes(tc, G1)
        if prev_tail is not None:
            prev_tail()
        first_after_T = {"v": True}

        def fm_group(k, e_, bk):
            if first_after_T["v"]:
                first_after_T["v"] = False
                for kc in range(KC):
                    op("pe", lambda e, kc=kc: e.matmul(banks[bk][:, 0:Tt],
                                                       lhsT=wbuf[:, k, kc * 512 + e_ * P:kc * 512 + (e_ + 1) * P],
                                                       rhs=xT[:, kc, 0:Tt], start=(kc == 0), stop=(kc == KC - 1)),
                       reads=[Rwbuf[k], RxT[kc]], writes=[Rbank[bk]], signal=(kc == KC - 1))
                return

            def mm(e):
                ins = None
                for kc in range(KC):
                    ins = e.matmul(banks[bk][:, 0:Tt], lhsT=wbuf[:, k, kc * 512 + e_ * P:kc * 512 + (e_ + 1) * P],
                                   rhs=xT[:, kc, 0:Tt], start=(kc == 0), stop=(kc == KC - 1))
                return ins
            op("pe", mm, reads=[Rwbuf[k]] + RxT, writes=[Rbank[bk]])

        def win_group(k, e_, gi):
            if is_t0 and not (16 <= gi < 20 or gi >= 28):
                def mm_meta(e, k=k, e_=e_, gi=gi):
                    ins = None
                    for kc in range(KC):
                        ins = e.matmul(banks[7][:, gi * 16:(gi + 1) * 16],
                                       lhsT=wbuf[:, k, kc * 512 + e_ * P:kc * 512 + (e_ + 1) * P],
                                       rhs=xTm[:, kc, :], start=(kc == 0), stop=(kc == KC - 1))
                    return ins
                op("pe", mm_meta, reads=[Rwbuf[k], RxTm], writes=[Rbank[7]])
            bk = next_bank()

            fm_group(k, e_, bk)
            return bk

        def pool_chunk(c, bk, gi):
            g = c // 2
            w = POOL_W[g]
            sl = c % 3
            ext = uext[:, sl, 0:EW]
            if is_t0:
                op("act", lambda e: e.activation(out=uh_meta[:, c, :], in_=banks[7][:, gi * 16 + 1:gi * 16 + 16],
                                                 func=AF.Copy), reads=[Rbank[7]], writes=[Ruhm[c]])
            ext3 = ext.rearrange("p (s w) -> p s w", s=nseg)
            if is_m:
                op("dve", lambda e: e.tensor_copy(out=ext3[:, 0:1, 0:15],
                                                  in_=uh[:, c, 0:15].rearrange("p (s w) -> p s w", s=1)),
                   reads=[Ruh[c]], writes=[Ruext[sl]])
                op("dve", lambda e: e.tensor_copy(out=ext3[:, 4:8, 0:15],
                                                  in_=uh_in[:, c, 0:60].rearrange("p (s w) -> p s w", s=4)),
                   reads=[Ruh_in], writes=[Ruext[sl]])
            else:
                op("dve", lambda e: e.tensor_copy(out=seg3(ext, nseg, 15 + L, 0, 15),
                                                  in_=uh_src[:, c, 0:nseg * 15].rearrange("p (s w) -> p s w", s=nseg)),
                   reads=[Ruh_src[c]], writes=[Ruext[sl]])
            op("act", lambda e: e.activation(out=seg3(ext, nseg, 15 + L, 15, 15 + L),
                                             in_=banks[bk][:, 0:Tt].rearrange("p (s l) -> p s l", s=nseg),
                                             func=AF.Copy), reads=[Rbank[bk]], writes=[Ruext[sl]])
            if is_m:
                op("dve", lambda e: e.tensor_copy(out=ext3[:, 1:4, 0:15], in_=ext3[:, 0:3, L:L + 15]),
                   reads=[Ruext[sl]], writes=[Ruext[sl]])
            op("dve", lambda e: e.tensor_copy(out=uh[:, c, 0:nseg * 15].rearrange("p (s w) -> p s w", s=nseg),
                                              in_=seg3(ext, nseg, 15 + L, L, 15 + L)),
               reads=[Ruext[sl]], writes=[Ruh[c]])
            cur, Rcur = ext, Ruext[sl]
            sh = 1
            pp = 0
            while sh < w:
                nxt = spp_[:, pp, 0:EW]
                op("dve", lambda e, cur=cur, nxt=nxt, sh=sh: e.tensor_tensor(out=nxt[:, sh:EW], in0=cur[:, sh:EW],
                                                                             in1=cur[:, 0:EW - sh], op=ALU.add),
                   reads=[Rcur], writes=[Rspq[pp]])
                cur, Rcur = nxt, Rspq[pp]
                pp ^= 1
                sh *= 2
            dsl = c
            op("dve", lambda e, cur=cur: e.scalar_tensor_tensor(
                out=dT[:, dsl, 0:Tt].rearrange("p (s l) -> p s l", s=nseg),
                in0=seg3(cur, nseg, 15 + L, 15, 15 + L), scalar=1.0 / w,
                in1=seg3(ext, nseg, 15 + L, 15, 15 + L), op0=ALU.mult, op1=ALU.subtract),
               reads=[Rcur, Ruext[sl]], writes=[RdT[dsl]])

        def pool_mm(g):
            for o in range(2):
                bk = next_bank()

                def mm(e, o=o, bk=bk):
                    ins = None
                    for kc in range(2):
                        ins = e.matmul(banks[bk][:, 0:Tt], lhsT=wpool[:, g, kc, o * P:(o + 1) * P],
                                       rhs=dT[:, 2 * g + kc, 0:Tt], start=(kc == 0), stop=(kc == 1))
                    return ins
                op("pe", mm, reads=[Rwpool_g[g], RdT[2 * g], RdT[2 * g + 1]], writes=[Rbank[bk]])
                c = 2 * g + o
                op("dve", lambda e, bk=bk, c=c: e.tensor_scalar(out=mixT[:, c, 0:Tt], in0=banks[bk][:, 0:Tt],
                                                                scalar1=cT[:, PSC + c:PSC + c + 1], scalar2=None,
                                                                op0=ALU.mult),
                   reads=[Rbank[bk], RcT], writes=[Rhid[c]])

        gi = 0
        for i in range(2):
            k = next_chunk()
            for e_ in range(4):
                bk = win_group(k, e_, gi)
                pool_chunk(i * 4 + e_, bk, gi)
                gi += 1
        pool_pending = [0, 1, 2, 3]
        for jh in range(2):
            k = next_chunk()
            for e_ in range(4):
                j = jh * 4 + e_
                bk = win_group(k, e_, gi)
                zext = czb(e_)[:, 0:ZW]
                if is_t0:
                    op("act", lambda e, j=j, gi=gi: e.activation(out=zmc[:, j, :],
                                                                 in_=banks[7][:, gi * 16 + 14:gi * 16 + 16],
                                                                 func=AF.Copy), reads=[Rbank[7]], writes=[Rzmc[j]])
                op("act", lambda e, zext=zext, bk=bk: e.activation(
                    out=seg3(zext, nseg, 2 + L, 2, 2 + L),
                    in_=banks[bk][:, 0:Tt].rearrange("p (s l) -> p s l", s=nseg), func=AF.Copy),
                   reads=[Rbank[bk]], writes=[Rcz[e_]])
                gi += 1
            k = next_chunk()
            for e_ in range(4):
                j = jh * 4 + e_
                bk = win_group(k, e_, gi)
                zext = czb(e_)[:, 0:ZW]
                if is_t0:
                    op("dve", lambda e, j=j, gi=gi: e.tensor_tensor(out=zh_meta[:, j, :], in0=zmc[:, j, :],
                                                                    in1=banks[7][:, gi * 16 + 14:gi * 16 + 16],
                                                                    op=ALU.mult),
                       reads=[Rbank[7], Rzmc[j]], writes=[Rzhm[j]])
                op("dve", lambda e, zext=zext, bk=bk: e.tensor_tensor(
                    out=seg3(zext, nseg, 2 + L, 2, 2 + L), in0=seg3(zext, nseg, 2 + L, 2, 2 + L),
                    in1=banks[bk][:, 0:Tt].rearrange("p (s l) -> p s l", s=nseg), op=ALU.mult),
                   reads=[Rbank[bk], Rcz[e_]], writes=[Rcz[e_]])
                if is_m:
                    zext3 = zext.rearrange("p (s w) -> p s w", s=nseg)
                    op("dve", lambda e, zext3=zext3, j=j: e.tensor_copy(
                        out=zext3[:, 0:1, 0:2], in_=zh[:, j, 0:2].rearrange("p (s w) -> p s w", s=1)),
                       reads=[Rzh[j]], writes=[Rcz[e_]])
                    op("dve", lambda e, zext3=zext3, j=j: e.tensor_copy(
                        out=zext3[:, 4:8, 0:2], in_=zh_in[:, j, 0:8].rearrange("p (s w) -> p s w", s=4)),
                       reads=[Rzh_in], writes=[Rcz[e_]])
                    op("dve", lambda e, zext3=zext3: e.tensor_copy(out=zext3[:, 1:4, 0:2],
                                                                   in_=zext3[:, 0:3, L:L + 2]),
                       reads=[Rcz[e_]], writes=[Rcz[e_]])
                else:
                    op("dve", lambda e, zext=zext, j=j: e.tensor_copy(
                        out=seg3(zext, nseg, 2 + L, 0, 2),
                        in_=zh_src[:, j, 0:nseg * 2].rearrange("p (s w) -> p s w", s=nseg)),
                       reads=[Rzh_src[j]], writes=[Rcz[e_]])
                op("dve", lambda e, zext=zext, j=j: e.tensor_copy(
                    out=zh[:, j, 0:nseg * 2].rearrange("p (s w) -> p s w", s=nseg),
                    in_=seg3(zext, nseg, 2 + L, L, 2 + L)),
                   reads=[Rcz[e_]], writes=[Rzh[j]])
                ac = accb(e_)
                op("dve", lambda e, zext=zext, ac=ac, j=j: e.tensor_scalar(
                    out=ac[:, 0:ZW - 2], in0=zext[:, 2:ZW], scalar1=cT[:, CW + 16 + j:CW + 16 + j + 1],
                    scalar2=None, op0=ALU.mult), reads=[Rcz[e_], RcT], writes=[Racc[e_]])
                op("dve", lambda e, zext=zext, ac=ac, j=j: e.scalar_tensor_tensor(
                    out=ac[:, 0:ZW - 2], in0=zext[:, 1:ZW - 1], scalar=cT[:, CW + 8 + j:CW + 8 + j + 1],
                    in1=ac[:, 0:ZW - 2], op0=ALU.mult, op1=ALU.add), reads=[Rcz[e_], RcT, Racc[e_]],
                   writes=[Racc[e_]])
                op("dve", lambda e, zext=zext, ac=ac, j=j: e.scalar_tensor_tensor(
                    out=ac[:, 0:ZW - 2], in0=zext[:, 0:ZW - 2], scalar=cT[:, CW + j:CW + j + 1],
                    in1=ac[:, 0:ZW - 2], op0=ALU.mult, op1=ALU.add), reads=[Rcz[e_], RcT, Racc[e_]],
                   writes=[Racc[e_]])
                gi += 1
                if pool_pending:
                    pool_mm(pool_pending.pop(0))
            k = next_chunk()
            for e_ in range(4):
                j = jh * 4 + e_
                bk = win_group(k, e_, gi)
                ac = accb(e_)
                if nseg == 1:
                    acv = ac[:, 0:Tt].rearrange("p (s l) -> p s l", s=1)
                else:
                    acv = ac[:, 0:nseg * (2 + L)].rearrange("p (s w) -> p s w", s=nseg)[:, :, 0:L]
                op("dve", lambda e, acv=acv, bk=bk, j=j: e.tensor_tensor(
                    out=mixT[:, 8 + j, 0:Tt].rearrange("p (s l) -> p s l", s=nseg), in0=acv,
                    in1=banks[bk][:, 0:Tt].rearrange("p (s l) -> p s l", s=nseg), op=ALU.mult),
                   reads=[Rbank[bk], Racc[e_]], writes=[Rhid[8 + j]])
                gi += 1
            if jh == 0 and not is_t0:
                for b in range(nblk):
                    op("act", lambda e, b=b: e.dma_start(out=x_tm[:, b, :], in_=tc["xblk"][b]),
                       writes=[Rx_tm[b]], dma_sem=dxt[b])

        for (is_u, c0, ncol, sdst) in tc["states"]:
            hsrc, Rh = (uh, Ruh) if is_u else (zh, Rzh)
            sbuf_stage, Rst = (stg, Rstg) if is_u else (stg2, [Rstg2])
            for half in range(2):
                bk = next_bank()

                def tr(e, half=half, bk=bk, hsrc=hsrc, c0=c0, ncol=ncol):
                    ins = None
                    for cc in range(4):
                        c = half * 4 + cc
                        ins = e.transpose(out=banks[bk][0:ncol, cc * P:(cc + 1) * P], in_=hsrc[:, c, c0:c0 + ncol],
                                          identity=idf[:])
                    return ins
                op("pe", tr, reads=[Rh[half * 4 + cc] for cc in range(4)] + [Ridf], writes=[Rbank[bk]])
                op("act", lambda e, half=half, bk=bk, sbuf_stage=sbuf_stage, ncol=ncol: e.activation(
                    out=sbuf_stage[0:ncol, half * 512:(half + 1) * 512], in_=banks[bk][0:ncol, :], func=AF.Copy),
                   reads=[Rbank[bk]], writes=Rst)
            store_tokens.append(op("act", lambda e, sbuf_stage=sbuf_stage, ncol=ncol, sdst=sdst: e.dma_start(
                out=sdst, in_=sbuf_stage[0:ncol, :]), reads=Rst, dma_sem=dstate[0 if is_u else 1]))

        if not is_t0:
            ring["n"] = 8
        align_ring(4)
        def evac_wout(b, n, bk):
            op("dve", lambda e: e.tensor_tensor(
                out=x_tm[:, b, n * 512:(n + 1) * 512], in0=x_tm[:, b, n * 512:(n + 1) * 512],
                in1=banks[bk][:, :], op=ALU.add), reads=[Rbank[bk], Rx_tm[b]], writes=[Rx_tm[b]])
            op("act", lambda e: e.activation(out=xn_tm[:, b, n * 512:(n + 1) * 512],
                                             in_=x_tm[:, b, n * 512:(n + 1) * 512], func=AF.Square,
                                             accum_out=ssp[:, b * 4 + n:b * 4 + n + 1]),
               reads=[Rx_tm[b]], writes=[Rxn[b], Rssp_b[b]])

        for n in range(3):
            k = next_chunk()
            bks = [next_bank() for _ in range(nblk)]

            def mm(e, k=k, bks=bks):
                ins = None
                for kc in range(KC):
                    for b in range(nblk):
                        ins = e.matmul(banks[bks[b]][:, :], lhsT=mixT[:, kc, b * P:(b + 1) * P],
                                       rhs=wbuf[:, k, kc * 512:(kc + 1) * 512], start=(kc == 0), stop=(kc == KC - 1))
                return ins
            op("pe", mm, reads=[Rwbuf[k]] + Rhid[0:16], writes=[Rbank[bk] for bk in bks])
            for b in range(nblk):
                evac_wout(b, n, bks[b])
        k = next_chunk()
        for b in range(nblk):
            bk = next_bank()

            def mm(e, k=k, b=b, bk=bk):
                ins = None
                for kc in range(KC):
                    ins = e.matmul(banks[bk][:, :], lhsT=mixT[:, kc, b * P:(b + 1) * P],
                                   rhs=wbuf[:, k, kc * 512:(kc + 1) * 512], start=(kc == 0), stop=(kc == KC - 1))
                return ins
            op("pe", mm, reads=[Rwbuf[k]] + Rhid[0:16], writes=[Rbank[bk]])
            evac_wout(b, 3, bk)
            op("dve", lambda e, b=b: e.tensor_tensor(out=ssp[:, b * 4:b * 4 + 2], in0=ssp[:, b * 4:b * 4 + 2],
                                                     in1=ssp[:, b * 4 + 2:b * 4 + 4], op=ALU.add),
               reads=[Rssp_b[b]], writes=[Rssp_b[b]])
            op("dve", lambda e, b=b: e.tensor_tensor(out=ss[:, 4 + b:5 + b], in0=ssp[:, b * 4:b * 4 + 1],
                                                     in1=ssp[:, b * 4 + 1:b * 4 + 2], op=ALU.add),
               reads=[Rssp_b[b]], writes=[Rss2[b]])
            rstd_from_ss(4 + b, 1, r_ss=Rss2[b], r_rstd=Rrstd2[b])
            if b % 2 == 0:
                op("dve", lambda e, b=b: e.tensor_scalar(out=xn_tm[:, b, :], in0=x_tm[:, b, :],
                                                         scalar1=rstd[:, 4 + b:5 + b], scalar2=None, op0=ALU.mult),
                   reads=[Rx_tm[b], Rrstd2[b]], writes=[Rxn[b]])
            else:
                op("act", lambda e, b=b: e.mul(out=xn_tm[:, b, :], in_=x_tm[:, b, :], mul=rstd[:, 4 + b:5 + b]),
                   reads=[Rx_tm[b], Rrstd2[b]], writes=[Rxn[b]])
        transposes(tc, G2)
        first_after_T["v"] = True

        for h in range(2):
            for m in range(8):
                k = next_chunk()
                for e_ in range(4):
                    hc = m * 4 + e_
                    bk = next_bank()

                    fm_group(k, e_, bk)
                    if h == 0 and next_tc is not None:
                        for kind_, b_ in SA_SCHED.get(hc, []):
                            (sa_load if kind_ == "l" else sa_compute)(next_tc, b_)
                    vs = hc % 2
                    op("act", lambda e, bk=bk, vs=vs: e.activation(out=vb[:, vs, 0:Tt], in_=banks[bk][:, 0:Tt],
                                                                   func=AF.Relu), reads=[Rbank[bk]], writes=[Rvb[vs]])
                    op("dve", lambda e, hc=hc, vs=vs: e.tensor_tensor(out=hid[:, hc, 0:Tt], in0=vb[:, vs, 0:Tt],
                                                                      in1=vb[:, vs, 0:Tt], op=ALU.mult),
                       reads=[Rvb[vs]], writes=[Rhid[hc]])
            if h == 1 and next_tc is not None:
                transposes(next_tc, G1)
            align_ring(4)
            for n in range(4):
                bks = [next_bank() for _ in range(nblk)]
                for q in range(2):
                    k = next_chunk()

                    def mm(e, k=k, q=q, bks=bks):
                        ins = None
                        for kc in range(KC):
                            for b in range(nblk):
                                ins = e.matmul(banks[bks[b]][:, :], lhsT=hid[:, q * 16 + kc, b * P:(b + 1) * P],
                                               rhs=wbuf[:, k, kc * 512:(kc + 1) * 512],
                                               start=(q == 0 and kc == 0), stop=(q == 1 and kc == KC - 1))
                        return ins
                    op("pe", mm, reads=[Rwbuf[k]] + Rhid[q * 16:(q + 1) * 16], writes=[Rbank[bk] for bk in bks])
                for b in range(nblk):
                    op("dve", lambda e, b=b, n=n, bks=bks: e.tensor_tensor(
                        out=x_tm[:, b, n * 512:(n + 1) * 512], in0=x_tm[:, b, n * 512:(n + 1) * 512],
                        in1=banks[bks[b]][:, :], op=ALU.add), reads=[Rbank[bks[b]], Rx_tm[b]], writes=[Rx_tm[b]])

        def tail():
            for b in range(nblk):
                op("act", lambda e, b=b: e.activation(out=junk[:], in_=x_tm[:, b, :], func=AF.Square,
                                                      accum_out=ss[:, 8 + b:9 + b]),
                   reads=[Rx_tm[b]], writes=[Rjunk, Rss])
            rstd_from_ss(8, nblk)
            for b in range(nblk):
                op("dve", lambda e, b=b: e.scalar_tensor_tensor(out=x_tm[:, b, :], in0=x_tm[:, b, :],
                                                                scalar=rstd[:, 8 + b:9 + b], in1=fgb[:],
                                                                op0=ALU.mult, op1=ALU.mult),
                   reads=[Rx_tm[b], Rrstd, Rfgb], writes=[Rx_tm[b]])
                store_tokens.append(op("act", lambda e, b=b: e.dma_start(out=tc["yblk"][b],
                                                                          in_=x_tm[:, b, :]),
                                       reads=[Rx_tm[b]], dma_sem=dst[b]))
        return tail

    for b_ in range(4):
        op("act", lambda e, b_=b_: e.activation(out=xn_tm[:, b_, :], in_=x_tm[:, b_, :], func=AF.Square,
                                                accum_out=ss[:, b_:b_ + 1]),
           reads=[Rx_tm[b_]], writes=[Rxn[b_], Rss])
        rstd_from_ss(b_, 1)
        op("dve", lambda e, b_=b_: e.tensor_scalar(out=xn_tm[:, b_, :], in0=x_tm[:, b_, :],
                                                   scalar1=rstd[:, b_:b_ + 1], scalar2=None, op0=ALU.mult),
           reads=[Rx_tm[b_], Rrstd], writes=[Rxn[b_]])
    tail = None
    for ti, tc in enumerate(tiles):
        tail = emit_tile(ti, tc, tiles[ti + 1] if ti + 1 < len(tiles) else None, tail)
    tail()
    assert wstate["used"] == TOTAL_CHUNKS, wstate

    op("sp", lambda e: (e.engine_nop() if hasattr(e, "engine_nop") else e.nop()), extra_waits=store_tokens, signal=False)

    with nc.Block() as block:
        @block.sync
        def _(e):
            S.replay("sp", e)

        @block.gpsimd
        def _(e):
            S.replay("pool", e)

        @block.scalar
        def _(e):
            S.replay("act", e)

        @block.vector
        def _(e):
            S.replay("dve", e)

        @block.tensor
        def _(e):
            S.replay("pe", e)
    S.close()
    es.close()
    return nc


def kernel(x_prompt, x_sample, cache_pool, cache_conv, meta_tokens, norm1_g, w_in, w_pool, pool_scale,
           conv_w, w_out, norm2_g, w_up, w_down, final_g):
    f = lambda a: np.ascontiguousarray(np.asarray(a, dtype=np.float32))
    x_prompt, x_sample, cache_pool, cache_conv = f(x_prompt), f(x_sample), f(cache_pool), f(cache_conv)
    cvec = np.concatenate([f(norm1_g).reshape(16, 128), f(norm2_g).reshape(16, 128),
                           f(pool_scale).reshape(8, 128), f(conv_w).reshape(24, 128)], axis=0)
    shared = {
        "meta": f(meta_tokens), "cvec": np.ascontiguousarray(cvec), "fg": f(final_g).reshape(1, D),
        "w_in": f(w_in)[0], "w_pool": f(w_pool)[0], "w_out": f(w_out)[0], "w_up": f(w_up)[0],
        "w_down": f(w_down)[0],
    }
    in_maps = []
    for c in range(N_CORES):
        m = dict(shared)
        m["xp"] = x_prompt[2 * c:2 * c + 2]
        m["xsm"] = x_sample[4 * c:4 * c + 4].reshape(128, D)
        m["cpool"] = cache_pool[0, 4 * c:4 * c + 4].reshape(60, 1024)
        m["cconv"] = cache_conv[0, 4 * c:4 * c + 4].reshape(8, 1024)
        in_maps.append(m)
    nc = build_program()
    res = run_bass_kernel_spmd(nc, in_maps, core_ids=list(range(N_CORES)))
    R = res.results
    g = lambda name: [np.asarray(R[c][name], dtype=np.float32) for c in range(N_CORES)]
    y_prompt = np.concatenate(g("yp"), axis=0)
    y_sample = np.concatenate([a.reshape(4, 32, D) for a in g("ysm")], axis=0)
    sp_pool = np.concatenate(g("spp"), axis=0)[None]
    sp_conv = np.concatenate(g("scp"), axis=0)[None]
    ss_pool = np.concatenate([a.reshape(4, 15, 1024) for a in g("sps")], axis=0)[None]
    ss_conv = np.concatenate([a.reshape(4, 2, 1024) for a in g("scs")], axis=0)[None]
    return (y_prompt, y_sample, sp_pool, sp_conv, ss_pool, ss_conv)
```
